# Optimizing a Trainium2 kernel written in Bass

```python
import jax, jax.numpy as jnp
from jax import lax
import numpy as np

D_MODEL = 1024
BATCH = 8
SEQ = 8192
DEPTH = 1
DEC_BATCH = 8
DEC_SEQ = 32
PAST_LEN = 1024

CHUNK = 64
D_LRU = 512
LRU_HEADS = 8
LRU_HEAD_DIM = D_LRU // LRU_HEADS
CONV_W = 4
LRU_C = 8.0
D_SG = 512
SG_GROUPS = 4
SG_GROUP_DIM = D_SG // SG_GROUPS
SG_LEN = 128
D_MIX = D_LRU + D_SG
D_IN = 2 * D_LRU + 2 * D_SG
D_FF = 2816
N_MOD = 9
ALPHA = (2 * DEPTH) ** 0.25
BETA = (8 * DEPTH) ** -0.25
LN_EPS = 1e-5

kernel_name = "hybrid_rglru_gmlp_streaming_step"


def layer_norm(x, g, b):
    xf = x.astype(jnp.float32)
    mu = jnp.mean(xf, axis=-1, keepdims=True)
    xc = xf - mu
    var = jnp.mean(xc * xc, axis=-1, keepdims=True)
    return (xc * lax.rsqrt(var + LN_EPS) * g.astype(jnp.float32) + b.astype(jnp.float32)).astype(x.dtype)


def swiglu(h, w1, w3, w2):
    return (jax.nn.silu(h @ w1) * (h @ w3)) @ w2


def causal_conv(x, hist, w, b):
    T = x.shape[1]
    xp = jnp.concatenate([hist.astype(x.dtype), x], axis=1)
    y = sum((w[k] * xp[:, k:k + T] for k in range(CONV_W)), b)
    return y, xp[:, -(CONV_W - 1):]


def rg_lru(x, h0, w_a, b_a, w_x, b_x, lam):
    B, T, _ = x.shape
    xh = x.reshape(B, T, LRU_HEADS, LRU_HEAD_DIM)
    r = jax.nn.sigmoid(jnp.einsum('bthi,hij->bthj', xh, w_a).reshape(B, T, D_LRU) + b_a)
    i = jax.nn.sigmoid(jnp.einsum('bthi,hij->bthj', xh, w_x).reshape(B, T, D_LRU) + b_x)
    log_a = -LRU_C * r.astype(jnp.float32) * jax.nn.softplus(-lam.astype(jnp.float32))
    a = jnp.exp(log_a)
    u = jnp.sqrt(-jnp.expm1(2.0 * log_a)) * (i * x).astype(jnp.float32)

    def combine(left, right):
        a1, b1 = left
        a2, b2 = right
        return a1 * a2, a2 * b1 + b2

    a_cum, h_cum = lax.associative_scan(combine, (a, u), axis=1)
    h = h_cum + a_cum * h0[:, None, :].astype(jnp.float32)
    return h.astype(x.dtype), h[:, -1].astype(x.dtype)


def spatial_gate(u, v, w_s, b_s):
    B, T, _ = v.shape
    n = -(-T // SG_LEN)
    pad = n * SG_LEN - T
    vp = jnp.pad(v, ((0, 0), (0, pad), (0, 0))).reshape(B, n, SG_LEN, SG_GROUPS, SG_GROUP_DIM)
    mask = jnp.tril(jnp.ones((SG_LEN, SG_LEN), dtype=bool))
    w = jnp.where(mask[None], w_s, jnp.zeros_like(w_s))
    z = jnp.einsum('gij,bcjgd->bcigd', w, vp) + jnp.transpose(b_s)[None, None, :, :, None]
    z = z.reshape(B, n * SG_LEN, D_SG)[:, :T]
    return u * z


def layer(x, c, conv_hist, h0, w_ada, b_ada, ln_g, ln_b,
          ffn1_w1, ffn1_w3, ffn1_w2, ffn2_w1, ffn2_w3, ffn2_w2,
          w_in, w_out, conv_w, conv_b, lru_wa, lru_ba, lru_wx, lru_bx, lru_lambda,
          sg_ln_g, sg_ln_b, sg_w, sg_b):
    B = x.shape[0]
    mod = (jax.nn.silu(c) @ w_ada + b_ada).reshape(B, N_MOD, D_MODEL)[:, None]

    def modulate(h, k):
        return h * (1.0 + mod[:, :, 3 * k + 1]) + mod[:, :, 3 * k]

    def gate(k):
        return 1.0 + mod[:, :, 3 * k + 2]

    f = swiglu(modulate(x, 0), ffn1_w1, ffn1_w3, ffn1_w2)
    x = layer_norm(ALPHA * x + 0.5 * gate(0) * f, ln_g[0], ln_b[0])

    proj = modulate(x, 1) @ w_in
    xa, ga, us, vs = jnp.split(proj, [D_LRU, 2 * D_LRU, 2 * D_LRU + D_SG], axis=-1)
    xc, new_hist = causal_conv(xa, conv_hist, conv_w, conv_b)
    hl, h_last = rg_lru(xc, h0, lru_wa, lru_ba, lru_wx, lru_bx, lru_lambda)
    ya = hl * jax.nn.gelu(ga)
    vn = layer_norm(vs, sg_ln_g, sg_ln_b)
    yb = spatial_gate(us, vn, sg_w, sg_b)
    m = jnp.concatenate([ya, yb], axis=-1) @ w_out
    x = layer_norm(ALPHA * x + gate(1) * m, ln_g[1], ln_b[1])

    f = swiglu(modulate(x, 2), ffn2_w1, ffn2_w3, ffn2_w2)
    x = layer_norm(ALPHA * x + 0.5 * gate(2) * f, ln_g[2], ln_b[2])
    return x, new_hist, h_last, vn


def setup_inputs(seed: int = 0) -> dict:
    key = jax.random.key(seed)
    ks = iter(jax.random.split(key, 40))
    nrm = lambda shape, s: jax.random.normal(next(ks), shape, jnp.float32) * s
    a0 = jax.random.uniform(next(ks), (DEPTH, D_LRU), jnp.float32, 0.9, 0.999)
    sig = a0 ** (1.0 / LRU_C)
    lru_lambda = jnp.log(sig) - jnp.log1p(-sig)
    return {
        "x_prompt": nrm((BATCH, SEQ, D_MODEL), 1.0),
        "x_sample": nrm((DEC_BATCH, DEC_SEQ, D_MODEL), 1.0),
        "c_prompt": nrm((BATCH, D_MODEL), 1.0),
        "c_sample": nrm((DEC_BATCH, D_MODEL), 1.0),
        "state_conv": nrm((DEPTH, DEC_BATCH, CONV_W - 1, D_LRU), 1.0),
        "state_lru": nrm((DEPTH, DEC_BATCH, D_LRU), 0.5),
        "ln_in_g": 1.0 + nrm((D_MODEL,), 0.02),
        "ln_in_b": nrm((D_MODEL,), 0.02),
        "w_ada": nrm((DEPTH, D_MODEL, N_MOD * D_MODEL), 0.2 * D_MODEL ** -0.5),
        "b_ada": nrm((DEPTH, N_MOD * D_MODEL), 0.02),
        "ln_g": 1.0 + nrm((DEPTH, 3, D_MODEL), 0.02),
        "ln_b": nrm((DEPTH, 3, D_MODEL), 0.02),
        "ffn1_w1": nrm((DEPTH, D_MODEL, D_FF), BETA * D_MODEL ** -0.5),
        "ffn1_w3": nrm((DEPTH, D_MODEL, D_FF), BETA * D_MODEL ** -0.5),
        "ffn1_w2": nrm((DEPTH, D_FF, D_MODEL), BETA * D_FF ** -0.5),
        "ffn2_w1": nrm((DEPTH, D_MODEL, D_FF), BETA * D_MODEL ** -0.5),
        "ffn2_w3": nrm((DEPTH, D_MODEL, D_FF), BETA * D_MODEL ** -0.5),
        "ffn2_w2": nrm((DEPTH, D_FF, D_MODEL), BETA * D_FF ** -0.5),
        "w_in": nrm((DEPTH, D_MODEL, D_IN), D_MODEL ** -0.5),
        "w_out": nrm((DEPTH, D_MIX, D_MODEL), BETA * D_MIX ** -0.5),
        "conv_w": nrm((DEPTH, CONV_W, D_LRU), CONV_W ** -0.5),
        "conv_b": nrm((DEPTH, D_LRU), 0.02),
        "lru_wa": nrm((DEPTH, LRU_HEADS, LRU_HEAD_DIM, LRU_HEAD_DIM), LRU_HEAD_DIM ** -0.5),
        "lru_ba": nrm((DEPTH, D_LRU), 0.02),
        "lru_wx": nrm((DEPTH, LRU_HEADS, LRU_HEAD_DIM, LRU_HEAD_DIM), LRU_HEAD_DIM ** -0.5),
        "lru_bx": nrm((DEPTH, D_LRU), 0.02),
        "lru_lambda": lru_lambda,
        "sg_ln_g": 1.0 + nrm((DEPTH, D_SG), 0.02),
        "sg_ln_b": nrm((DEPTH, D_SG), 0.02),
        "sg_w": nrm((DEPTH, SG_GROUPS, SG_LEN, SG_LEN), SG_LEN ** -0.5),
        "sg_b": 1.0 + nrm((DEPTH, SG_GROUPS, SG_LEN), 0.02),
    }


def reference(x_prompt, x_sample, c_prompt, c_sample, state_conv, state_lru,
              ln_in_g, ln_in_b, w_ada, b_ada, ln_g, ln_b,
              ffn1_w1, ffn1_w3, ffn1_w2, ffn2_w1, ffn2_w3, ffn2_w2,
              w_in, w_out, conv_w, conv_b, lru_wa, lru_ba, lru_wx, lru_bx, lru_lambda,
              sg_ln_g, sg_ln_b, sg_w, sg_b):
    def run(x, c, conv_hists, h0s):
        x = layer_norm(x, ln_in_g, ln_in_b)
        hists, hs, vs = [], [], []
        for l in range(DEPTH):
            x, nh, hl, vn = layer(
                x, c, conv_hists[l], h0s[l], w_ada[l], b_ada[l], ln_g[l], ln_b[l],
                ffn1_w1[l], ffn1_w3[l], ffn1_w2[l], ffn2_w1[l], ffn2_w3[l], ffn2_w2[l],
                w_in[l], w_out[l], conv_w[l], conv_b[l], lru_wa[l], lru_ba[l],
                lru_wx[l], lru_bx[l], lru_lambda[l], sg_ln_g[l], sg_ln_b[l], sg_w[l], sg_b[l])
            hists.append(nh)
            hs.append(hl)
            vs.append(vn)
        return x, jnp.stack(hists), jnp.stack(hs), jnp.stack(vs)

    Bp = x_prompt.shape[0]
    zero_hist = jnp.zeros((DEPTH, Bp, CONV_W - 1, D_LRU), x_prompt.dtype)
    zero_h = jnp.zeros((DEPTH, Bp, D_LRU), x_prompt.dtype)
    y_prompt, conv_p, lru_p, _ = run(x_prompt, c_prompt, zero_hist, zero_h)
    y_sample, conv_s, lru_s, v_s = run(x_sample, c_sample, state_conv, state_lru)
    return (y_prompt, y_sample, conv_p, lru_p, conv_s, lru_s, v_s)
```

```python
import numpy as np
from contextlib import ExitStack
import concourse.bass as bass
import concourse.mybir as mybir
from concourse.bass_utils import run_bass_kernel_spmd

F32 = mybir.dt.float32
BF16 = mybir.dt.bfloat16
AF = mybir.ActivationFunctionType
ALU = mybir.AluOpType

D = 1024
DFF = 2816
NM = DFF // 128
DL = 512
SEQ = 8192
DEC = 32
ALPHA = 2.0 ** 0.25
EPS = 1e-5
NTP = 512
RING = 5
import os as _os
EARLY_T1 = False
SERIALIZE = bool(_os.environ.get("KSERIAL"))
SLOT = 4096

CH_U = 2 * 2 * 8 * 128
CH_D = NM * 128
CH_I = 4 * 8 * 128
CH_V = 8 * 512
CH_O = 4 * 8 * 128


def _chunk_table():
    tab = {}
    off = 0
    order = []
    for f in range(2):
        for cp in range(11):
            tab[("U", f, cp)] = (off, CH_U); order.append(("U", f, cp)); off += CH_U
        for n in range(8):
            tab[("D", f, n)] = (off, CH_D); order.append(("D", f, n)); off += CH_D
        if f == 0:
            for ci in range(3):
                tab[("I", ci)] = (off, CH_I); order.append(("I", ci)); off += CH_I
            tab[("V",)] = (off, CH_V); order.append(("V",)); off += CH_V
            for ci in range(2):
                tab[("O", ci)] = (off, CH_O); order.append(("O", ci)); off += CH_O
    return tab, order, off


CHUNKS, CH_ORDER, WTOT = _chunk_table()
N_PIECES = 12


def _pieces():
    per = (len(CH_ORDER) + N_PIECES - 1) // N_PIECES
    pcs = []
    piece_of = {}
    for i in range(0, len(CH_ORDER), per):
        grp = CH_ORDER[i:i + per]
        a = CHUNKS[grp[0]][0]
        b = CHUNKS[grp[-1]][0] + CHUNKS[grp[-1]][1]
        for k in grp:
            piece_of[k] = len(pcs)
        pcs.append((a, b))
    return pcs, piece_of


PIECES, PIECE_OF = _pieces()


class _Op:
    __slots__ = ("eng", "fn", "deps", "sig", "cnt", "dsem", "dval")


class Sched:
    ENGS = ("pe", "act", "dve", "pool", "sp")

    def __init__(self):
        self.q = {e: [] for e in self.ENGS}
        self.lastw = {}
        self.rd = {}
        self.dcnt = {}
        self.pe_trailer = None

    def mark(self, name):
        import os
        if os.environ.get("KSTOP") == name:
            self.cut = True

    def op(self, eng, fn, r=(), w=(), dma=None):
        if getattr(self, "cut", False):
            return None
        o = _Op()
        o.eng = eng; o.fn = fn; o.sig = False; o.cnt = 0; o.dsem = None; o.dval = 0
        if dma is not None:
            self.dcnt[dma] = self.dcnt.get(dma, 0) + 16
            o.dsem = dma; o.dval = self.dcnt[dma]
        deps = {}

        def add(d, raw):
            if d.dsem is None and d.eng == eng:
                if eng == "pe" or not raw:
                    return
            deps[id(d)] = d
        for k in r:
            d = self.lastw.get(k)
            if d is not None:
                add(d, True)
        for k in w:
            d = self.lastw.get(k)
            if d is not None:
                add(d, False)
            for d2 in self.rd.get(k, {}).values():
                add(d2, False)
        import os
        if SERIALIZE and getattr(self, "prev", None) is not None:
            deps[id(self.prev)] = self.prev
        self.prev = o
        o.deps = list(deps.values())
        for d in o.deps:
            if d.dsem is None:
                d.sig = True
        rk = o.dsem if o.dsem is not None else eng
        for k in r:
            self.rd.setdefault(k, {})[rk] = o
        for k in w:
            self.lastw[k] = o
            self.rd[k] = {}
        self.q[eng].append(o)
        return o

    def fence(self, keys, sem):
        o = _Op()
        o.eng = "sp"; o.fn = None; o.sig = False; o.cnt = 0; o.deps = []
        o.dsem = sem; o.dval = self.dcnt[sem]
        for k in keys:
            self.lastw[k] = o
            self.rd[k] = {}

    def number(self):
        for e in self.ENGS:
            c = 0
            for o in self.q[e]:
                if o.sig and o.dsem is None:
                    c += 1
                    o.cnt = c

    def emit(self, ename, eng, sems, final_waits=()):
        waited = {}
        for o in self.q[ename]:
            need = {}
            for d in o.deps:
                if d.dsem is not None:
                    s, v = d.dsem, d.dval
                else:
                    s, v = "E_" + d.eng, d.cnt
                if need.get(s, 0) < v:
                    need[s] = v
            pend = []
            for s, v in need.items():
                if waited.get(s, 0) < v:
                    pend.append((s, v))
                    waited[s] = v
            import os
            attach = (ename == "pe" and os.environ.get("KATTACH") and len(pend) >= 1)
            if attach:
                for s, v in pend[:-1]:
                    eng.wait_ge(sems[s], v)
                ins = o.fn(eng)
                ins._wait_ge(sems[pend[-1][0]], pend[-1][1])
            else:
                for s, v in pend:
                    eng.wait_ge(sems[s], v)
                ins = o.fn(eng)
            if o.dsem is not None:
                ins.then_inc(sems[o.dsem], 16)
            elif o.sig:
                tr = getattr(self, "trailers", {}).get(ename)
                if tr is not None:
                    ins = tr(eng)
                ins.then_inc(sems["E_" + o.eng], 1)
        for s in final_waits:
            if self.dcnt.get(s, 0) > 0:
                eng.wait_ge(sems[s], self.dcnt[s])


class SubTile:
    pass


def C(name, *a, **k):
    return lambda e: getattr(e, name)(*a, **k)


def build_nc(T):
    nc = bass.Bass("TRN2", target_bir_lowering=False)
    NPASS = T // NTP

    def din(name, shape, dt=F32):
        return nc.dram_tensor(name, list(shape), dt, kind="ExternalInput").ap()

    def dout(name, shape, dt=F32):
        return nc.dram_tensor(name, list(shape), dt, kind="ExternalOutput").ap()

    xp_d = din("xp", [T, D])
    xs_d = din("xs", [DEC, D])
    cc_d = din("cc", [128, 16])
    sconv_d = din("sconv", [3, DL])
    slru_d = din("slru", [DL])
    wada_d = din("wada", [D, 9 * D])
    bada_d = din("bada", [128, 72])
    lnp_d = din("lnp", [128, 64])
    lrup_d = din("lrup", [128, 32])
    wag_d = din("wag", [128, 4 * 64])
    wxg_d = din("wxg", [128, 4 * 64])
    sgw_d = din("sgw", [128, 4 * 128])
    sgb_d = din("sgb", [1, 512])
    sgln_d = din("sgln", [2, 512])
    ln3_d = din("ln3", [2, D])
    wall_d = din("wall", [128, WTOT])

    yp_d = dout("yp", [T, D])
    ys_d = dout("ys", [DEC, D])
    convp_d = dout("convp", [3, DL])
    lrup_o = dout("lrupo", [DL])
    convs_d = dout("convs", [3, DL])
    lrus_o = dout("lruso", [DL])
    vs_d = dout("vs", [DEC, DL])

    scr_d = nc.dram_tensor("scr", [128, WTOT], BF16, kind="Internal").ap()

    import os
    S = Sched()
    es = ExitStack()

    def sb(name, shape, dt=F32):
        return es.enter_context(nc.sbuf_tensor(name, list(shape), dt))[:]

    ring = sb("ring", [128, RING, SLOT], BF16)
    xin = sb("xin", [128, 3, D])
    tmr = sb("tmr", [128, 4, D])
    tmp = sb("tmp", [128, 3, 512])
    mt = sb("mt", [128, 2, 6, 512])
    xcb = sb("xcb", [128, 2, 512], BF16)
    ident = sb("ident", [128, 128])
    lnp = sb("lnp_s", [128, 8, 8])
    lrup = sb("lrup_s", [128, 4, 8])
    csp = sb("csp", [128, 4])
    wstg = sb("wstg", [128, 4, 64])
    wabd = sb("wabd", [128, 4, 128], BF16)
    wxbd = sb("wxbd", [128, 4, 128], BF16)
    wsT = sb("wsT", [128, 4, 128], BF16)
    sgb = sb("sgb_s", [128, 4, 128])
    sgG = sb("sgG", [128, 512])
    sgB = sb("sgB", [128, 512])
    g3bc = sb("g3bc", [128, D])
    b3bc = sb("b3bc", [128, D])
    cc = sb("cc_s", [128, 8, 2])
    csil = sb("csil", [128, 8, 2])
    bada = sb("bada_s", [128, 72])
    mod = sb("mod", [128, 2, 72])
    opsc = sb("opsc", [128, 8])
    gA = sb("gA", [128, 3, 8])
    bA = sb("bA", [128, 3, 8])
    gM = sb("gM", [128, 2, 3, 8])
    bM = sb("bM", [128, 2, 3, 8])
    gsc = sb("gsc", [128, 2, 3, 8])
    epst = sb("epst", [128, 1])
    onet = sb("onet", [128, 1])
    junk = sb("junk", [128, 4])
    tinyt = sb("tinyt", [128, 1])
    st6 = sb("st6", [128, 2, 2, 6])
    mv = sb("mv", [128, 2, 2])
    sdv = sb("sdv", [128, 2, 1])
    rsv = sb("rsv", [128, 2, 1])
    nmr = sb("nmr", [128, 2, 1])

    def mkset(pfx, NT, GS):
        st = SubTile()
        st.pfx = pfx; st.NT = NT; st.GS = GS; st.NG = NT // GS
        st.xa = sb(pfx + "xa", [128, 8, NT])
        st.xm = sb(pfx + "xm", [128, 8, NT], BF16)
        hr = sb(pfx + "hr", [128, 3 * 4 * NT + 12 + 4])
        st.h = hr[:, 0:NM * NT // 2].bitcast(BF16).rearrange("p (m t) -> p m t", m=NM)
        o = 0
        st.xpre = hr[:, o:o + 4 * (NT + 3)].rearrange("p (m t) -> p m t", m=4); o += 4 * (NT + 3)
        st.gg = hr[:, o:o + 4 * NT].rearrange("p (m t) -> p m t", m=4); o += 4 * NT
        st.us = hr[:, o:o + 4 * NT].rearrange("p (m t) -> p m t", m=4)
        st.vnb = sb(pfx + "vnb", [128, st.NG, 512], BF16)
        st.ymix = sb(pfx + "ymix", [128, 8, NT], BF16)
        st.hst = sb(pfx + "hst", [128, 4])
        st.hist = sb(pfx + "hist", [128, 4, 3])
        return st

    P = mkset("P", NTP, 128); P.s = 0; P.GSr = 128; P.NTr = NTP
    Sm = mkset("S", 128, 128); Sm.s = 1; Sm.GSr = DEC; Sm.NTr = DEC

    ps = [es.enter_context(nc.psum_tensor(f"ps{i}", [128, 512], F32))[:] for i in range(8)]
    bank_ctr = [0]

    held_banks = set()

    def nb():
        while True:
            b = bank_ctr[0] % 8
            bank_ctr[0] += 1
            if b not in held_banks:
                return b

    def release_banks(halves):
        for _, k in halves:
            if isinstance(k, tuple) and k[0] == "ps":
                held_banks.discard(k[1])

    tmp_ctr = [0]

    def ntmp():
        i = tmp_ctr[0] % 3
        tmp_ctr[0] += 1
        return i

    stat_ctr = [0]
    tmr_ctr = [0]

    def ntmr():
        i = tmr_ctr[0] % 4
        tmr_ctr[0] += 1
        return i

    ring_ctr = [0]

    CK = []

    def cload(key, out_ap, in_ap):
        S.op("sp", C("dma_start", out=out_ap, in_=in_ap, allow_slow_non_contiguous=True),
             w=(key,), dma="cst")
        CK.append(key)

    cload("cc", cc.rearrange("p k s -> p (k s)"), cc_d)
    cload("bada", bada, bada_d)
    cload("lnp", lnp.rearrange("p v c -> p (v c)"), lnp_d)
    cload("lrup", lrup.rearrange("p m q -> p (m q)"), lrup_d)
    cload("sgbraw", sgb.rearrange("p g i -> p (g i)"), sgb_d.partition_broadcast(128))
    cload("sgG", sgG, sgln_d[0:1, :].partition_broadcast(128))
    cload("sgB", sgB, sgln_d[1:2, :].partition_broadcast(128))
    cload("g3bc", g3bc, ln3_d[0:1, :].partition_broadcast(128))
    cload("b3bc", b3bc, ln3_d[1:2, :].partition_broadcast(128))
    S.fence(CK, "cst")

    for (src_d, dst, nm) in ((wag_d, wabd, "wa"), (wxg_d, wxbd, "wx")):
        S.op("sp", C("dma_start", out=wstg.rearrange("p m j -> p (m j)"), in_=src_d),
             w=("wstg",), dma="cst2" + nm)
        S.op("pool", C("memset", dst, 0.0), w=(nm + "bd",))
        S.op("dve", C("tensor_copy", out=dst[0:64, :, 0:64], in_=wstg[0:64, :, :]),
             r=("wstg",), w=(nm + "bd",))
        S.op("dve", C("tensor_copy", out=dst[64:128, :, 64:128], in_=wstg[64:128, :, :]),
             r=("wstg",), w=(nm + "bd",))
    sgst = tmp[:, 0, :].rearrange("p (g i) -> p g i", g=4)
    S.op("sp", C("dma_start", out=tmp[:, 0, :], in_=sgw_d), w=(("tmp", 0),), dma="cst3")
    for g in range(4):
        S.op("pool", C("affine_select", out=sgst[:, g, :], in_=sgst[:, g, :], pattern=[[1, 128]],
                                                    compare_op=ALU.is_ge, fill=0.0, base=0, channel_multiplier=-1),
             r=(("tmp", 0),), w=(("tmp", 0),))
    S.op("dve", C("tensor_copy", out=wsT, in_=sgst), r=(("tmp", 0),), w=("wsT",))
    S.op("pool", C("memset", ident, 1.0), w=("ident",))
    S.op("pool", C("affine_select", out=ident, in_=ident, pattern=[[1, 128]], compare_op=ALU.is_equal,
                                           fill=0.0, base=0, channel_multiplier=-1),
         r=("ident",), w=("ident",))
    S.op("pool", C("memset", epst, EPS), w=("eps",))
    S.op("pool", C("memset", P.hst, 0.0), w=(("Phst", 0), ("Phst", 1), ("Phst", 2), ("Phst", 3)))
    S.op("pool", C("memset", P.hist, 0.0), w=tuple(("Phist", m) for m in range(4)))
    S.op("pool", C("memset", onet, 1.0), w=("onet",))
    S.op("pool", C("memset", tinyt, 1e-20), w=("tiny",))
    S.op("act", C("activation", out=csp, in_=lrup[:, :, 7], func=AF.Exp, scale=-1.0), r=("lrup",), w=("csp",))
    S.op("act", C("activation", out=csp, in_=csp, func=AF.Ln, bias=onet, scale=1.0), r=("csp", "onet"), w=("csp",))
    S.op("dve", C("tensor_scalar", out=csp, in0=csp, scalar1=-8.0, scalar2=None, op0=ALU.mult),
         r=("csp",), w=("csp",))
    S.op("act", C("activation", out=csil, in_=cc, func=AF.Silu), r=("cc",), w=("csil",))

    xin_ctr = [0]
    P.xslots = {}

    def load_xg(st, t0, g, src_d):
        GSr = st.GSr
        slot = xin_ctr[0] % 3
        xin_ctr[0] += 1
        st.xslots[(t0, g)] = slot
        dst = xin[:, slot, :]
        key = ("xin", slot)
        keys = (key, (key, 0), (key, 1))
        if GSr < st.GS:
            S.op("pool", C("memset", dst, 0.0), w=keys)
        S.op("sp", C("dma_start", out=dst[:GSr, :], in_=src_d[t0 + g * GSr:t0 + (g + 1) * GSr, :]),
             w=keys, dma=f"xin{slot}")

    Sm.xslots = {}
    if not os.environ.get("KONLYS"):
        for g in range(3):
            load_xg(P, 0, g, xp_d)

    def cast_piece(i, extra_r=()):
        a, b = PIECES[i]
        S.op("pool", C("dma_start", out=scr_d[:, a:b], in_=wall_d[:, a:b], max_dma_last_dim=8192),
             r=extra_r, w=(("scr", i),), dma=f"cast{i}")

    N_EARLY = 1
    for i in range(N_EARLY):
        cast_piece(i)

    wst32 = ring.rearrange("p r s -> p (r s)").bitcast(F32)
    psM = ps[7]
    for q in range(18):
        sl = q % 2
        wv = wst32[:, sl * 4096:(sl + 1) * 4096].rearrange("p (k n) -> p k n", k=8)
        S.op("sp", C("dma_start",
            out=wv, in_=wada_d[:, q * 512:(q + 1) * 512].rearrange("(k p) n -> p k n", p=128)),
            w=(("ws", 2 * sl), ("ws", 2 * sl + 1)) + ((("wada_done", sl),) if q >= 16 else ()), dma=f"ws{2 * sl}")
        bq = q % 4
        for kc in range(8):
            S.op("pe", C("matmul", ps[bq][0:2, :], lhsT=csil[:, kc, :], rhs=wv[:, kc, :], start=(kc == 0), stop=(kc == 7)),
                 r=(("ws", 2 * sl), ("ws", 2 * sl + 1), "csil"), w=(("ps", bq),))
        ti = ntmp()
        S.op("act", C("activation", out=tmp[0:2, ti, :], in_=ps[bq][0:2, :], func=AF.Identity),
             r=(("ps", bq),), w=(("tmp", ti),))
        for jj in range(4):
            j = 4 * q + jj
            S.op("pe", C("transpose", psM[:, 2 * j:2 * j + 2], tmp[0:2, ti, jj * 128:(jj + 1) * 128], ident[0:2, 0:2]),
                 r=(("tmp", ti), "ident"), w=(("ps", 7),))
    for i in range(N_EARLY, len(PIECES)):
        cast_piece(i, extra_r=(("wada_done", 0), ("wada_done", 1)))
    psMv = psM[:, 0:144].rearrange("p (j s) -> p s j", s=2)
    for s in range(2):
        S.op("dve", C("tensor_tensor", out=mod[:, s, :], in0=psMv[:, s, :], in1=bada, op=ALU.add),
             r=(("ps", 7), "bada"), w=("mod",))
    for l in range(3):
        gpre = lnp[:, 2 * l, :]
        bpre = lnp[:, 2 * l + 1, :]
        S.op("dve", C("tensor_scalar", out=gA[:, l, :], in0=gpre, scalar1=ALPHA, scalar2=None, op0=ALU.mult),
             r=("lnp",), w=("modsc",))
        S.op("dve", C("tensor_scalar", out=bA[:, l, :], in0=bpre, scalar1=ALPHA, scalar2=None, op0=ALU.mult),
             r=("lnp",), w=("modsc",))
        for s in range(2):
            mv_ = mod[:, s, :].rearrange("p (k c) -> p k c", k=9)
            S.op("dve", C("tensor_scalar", out=opsc, in0=mv_[:, 3 * l + 1, :], scalar1=1.0, scalar2=None, op0=ALU.add),
                 r=("mod",), w=("opsc",))
            S.op("dve", C("tensor_tensor", out=gM[:, s, l, :], in0=gpre, in1=opsc, op=ALU.mult),
                 r=("opsc", "lnp"), w=("modsc",))
            S.op("dve", C("tensor_tensor", out=bM[:, s, l, :], in0=bpre, in1=opsc, op=ALU.mult),
                 r=("opsc", "lnp"), w=("modsc",))
            S.op("dve", C("tensor_tensor", out=bM[:, s, l, :], in0=bM[:, s, l, :], in1=mv_[:, 3 * l, :], op=ALU.add),
                 r=("modsc", "mod"), w=("modsc",))
            S.op("dve", C("tensor_scalar", out=gsc[:, s, l, :], in0=mv_[:, 3 * l + 2, :], scalar1=1.0,
                                                                      scalar2=(0.5 if l != 1 else 1.0), op0=ALU.add, op1=ALU.mult),
                 r=("mod",), w=("modsc",))

    if os.environ.get("KDBG") == "mod":
        S.op("sp", C("dma_start", out=yp_d[0:128, 0:144], in_=mod.rearrange("p s j -> p (s j)")), r=("mod",), dma="dbg")
        S.op("sp", C("dma_start", out=yp_d[128:256, 0:16], in_=csil.rearrange("p k s -> p (k s)")), r=("csil",), dma="dbg")
        S.op("sp", C("dma_start", out=yp_d[256:384, 0:4], in_=csp), r=("csp",), dma="dbg")
        S.op("sp", C("dma_start", out=yp_d[256:384, 8:32], in_=gA.rearrange("p l c -> p (l c)")), r=("modsc",), dma="dbg")
        S.op("sp", C("dma_start", out=yp_d[256:384, 32:56], in_=bA.rearrange("p l c -> p (l c)")), r=("modsc",), dma="dbg")
        S.op("sp", C("dma_start", out=yp_d[256:384, 64:112], in_=gM.rearrange("p s l c -> p (s l c)")), r=("modsc",), dma="dbg")
        S.op("sp", C("dma_start", out=yp_d[256:384, 128:176], in_=bM.rearrange("p s l c -> p (s l c)")), r=("modsc",), dma="dbg")
        S.op("sp", C("dma_start", out=yp_d[256:384, 192:240], in_=gsc.rearrange("p s l c -> p (s l c)")), r=("modsc",), dma="dbg")
        S.op("sp", C("dma_start", out=yp_d[256:384, 256:320], in_=lnp.rearrange("p v c -> p (v c)")), r=("lnp",), dma="dbg")
    S.mark("prologue")
    def wchunk(key):
        off, ln = CHUNKS[key]
        slot = ring_ctr[0] % RING
        ring_ctr[0] += 1
        S.op("sp", C("dma_start", out=ring[:, slot, 0:ln], in_=scr_d[:, off:off + ln]),
             r=(("scr", PIECE_OF[key]),), w=(("ws", slot),), dma=f"ws{slot}")
        return ring[:, slot, :], ("ws", slot)

    def xa_keys(st, c):
        return tuple((st.pfx + "xa", c, g) for g in range(st.NG))

    def ln_T1_half(st, g, half):
        pfx, GS = st.pfx, st.GS
        gsl = slice(g * GS, (g + 1) * GS)
        b = nb()
        for c4 in range(4):
            c = half * 4 + c4
            S.op("pe", C("transpose", ps[b][:GS, c4 * 128:(c4 + 1) * 128], st.xa[:, c, gsl], ident),
                 r=((pfx + "xa", c, g), "ident"), w=(("ps", b),))
        return (ps[b][:GS, :], ("ps", b))

    def ln_T1_early(st):
        st.early = {g: ln_T1_half(st, g, 0) for g in range(st.NG)}
        for _, k in st.early.values():
            held_banks.add(k[1])

    def ln_T1(st, g):
        early = getattr(st, "early", None)
        h0 = early.pop(g) if early and g in early else ln_T1_half(st, g, 0)
        return [h0, ln_T1_half(st, g, 1)]

    def ln_xin(st, g, t0):
        slot = st.xslots[(t0, g)]
        src = xin[:, slot, :]
        skey = ("xin", slot)
        GS = st.GS
        return [(src[:GS, 0:512], skey), (src[:GS, 512:1024], skey)], src, skey

    def ln_norm(st, g, halves, dst, dkey):
        GS = st.GS
        hb = stat_ctr[0] % 2
        stat_ctr[0] += 1
        for half, (hap, hkey) in enumerate(halves):
            S.op("dve", C("bn_stats", st6[:GS, hb, half, :], hap), r=(hkey,), w=(("st6", hb),))
        S.op("dve", C("bn_aggr", mv[:GS, hb, :], st6[:GS, hb, :, :].rearrange("p a b -> p (a b)")),
             r=(("st6", hb),), w=(("mv", hb),))
        S.op("act", C("activation", out=sdv[:GS, hb, :], in_=mv[:GS, hb, 1:2], func=AF.Sqrt, bias=epst[:GS, :], scale=1.0),
             r=(("mv", hb), "eps"), w=(("sdv", hb),))
        S.op("dve", C("reciprocal", rsv[:GS, hb, :], sdv[:GS, hb, :]), r=(("sdv", hb),), w=(("rsv", hb),))
        S.op("dve", C("scalar_tensor_tensor", out=nmr[:GS, hb, :], in0=mv[:GS, hb, 0:1], scalar=-1.0, in1=rsv[:GS, hb, :],
                      op0=ALU.mult, op1=ALU.mult),
             r=(("mv", hb), ("rsv", hb)), w=(("nmr", hb),))
        for half, (hap, hkey) in enumerate(halves):
            S.op("act", C("activation", out=dst[:GS, half * 512:(half + 1) * 512], in_=hap, func=AF.Identity,
                          bias=nmr[:GS, hb, :], scale=rsv[:GS, hb, :]),
                 r=(hkey, ("rsv", hb), ("nmr", hb)), w=((dkey, half),))
        release_banks(halves)

    def ln_T2(st, g, dst, dkey, l_next):
        pfx, GS, s = st.pfx, st.GS, st.s
        gsl = slice(g * GS, (g + 1) * GS)
        for half in range(2):
            b = nb()
            for c4 in range(4):
                c = half * 4 + c4
                S.op("pe", C("transpose", ps[b][:, c4 * GS:(c4 + 1) * GS], dst[:GS, c * 128:(c + 1) * 128], ident[:GS, :GS]),
                     r=((dkey, half), "ident"), w=(("ps", b),))
            for c4 in range(4):
                c = half * 4 + c4
                S.op("act", C("activation", out=st.xa[:, c, gsl], in_=ps[b][:, c4 * GS:(c4 + 1) * GS], func=AF.Identity,
                              bias=bA[:, l_next, c:c + 1], scale=gA[:, l_next, c:c + 1]),
                     r=(("ps", b), "modsc"), w=((pfx + "xa", c, g),))
            for c4 in range(4):
                c = half * 4 + c4
                S.op("dve", C("tensor_scalar", out=st.xm[:, c, gsl], in0=ps[b][:, c4 * GS:(c4 + 1) * GS],
                              scalar1=gM[:, s, l_next, c:c + 1], scalar2=bM[:, s, l_next, c:c + 1],
                              op0=ALU.mult, op1=ALU.add),
                     r=(("ps", b), "modsc") + tuple((pfx + "xa", half * 4 + q, g) for q in range(4)), w=((pfx + "xm", c),))

    def ln_out(st, g, dst, dkey, slot, t0):
        GS = st.GS
        for half in range(2):
            hs = slice(half * 512, (half + 1) * 512)
            S.op("pool", C("tensor_tensor", out=dst[:GS, hs], in0=dst[:GS, hs], in1=g3bc[:GS, hs], op=ALU.mult),
                 r=((dkey, half), "g3bc"), w=((dkey, half),))
            S.op("pool", C("tensor_tensor", out=dst[:GS, hs], in0=dst[:GS, hs], in1=b3bc[:GS, hs], op=ALU.add),
                 r=((dkey, half), "b3bc"), w=((dkey, half),))
        yd = yp_d if st is P else ys_d
        S.op("pool", C("dma_start", out=yd[t0 + g * st.GSr:t0 + (g + 1) * st.GSr, :], in_=dst[:st.GSr, :]),
             r=((dkey, 0), (dkey, 1)), dma=f"tmr{slot}")

    def ln_mid(st, l_next):
        pend = []
        for g in range(st.NG):
            halves = ln_T1(st, g)
            slot = ntmr()
            dst, dkey = tmr[:, slot, :], ("tmr", slot)
            ln_norm(st, g, halves, dst, dkey)
            pend.append((g, dst, dkey))
        for it in pend:
            ln_T2(st, *it, l_next)

    def resid(st, n, b, l):
        pfx, NT, s = st.pfx, st.NT, st.s
        S.op("dve", C("scalar_tensor_tensor", out=st.xa[:, n, :], in0=ps[b][:, :NT], scalar=gsc[:, s, l, n:n + 1], in1=st.xa[:, n, :],
                                                     op0=ALU.mult, op1=ALU.add),
             r=(("ps", b), "modsc") + xa_keys(st, n), w=xa_keys(st, n))

    def ffn(sts, f, l, after_u=None):
        for cp in range(11):
            wap, wkey = wchunk(("U", f, cp))
            wv = wap.rearrange("p (mm a k j) -> p mm a k j", mm=2, a=2, k=8)
            for st in sts:
                pfx, NT = st.pfx, st.NT
                for mm in range(2):
                    m = 2 * cp + mm
                    ba_, bb_ = nb(), nb()
                    for a, b in ((0, ba_), (1, bb_)):
                        for kc in range(8):
                            S.op("pe", C("matmul",
                                ps[b][:, :NT], lhsT=wv[:, mm, a, kc, :], rhs=st.xm[:, kc, :], start=(kc == 0), stop=(kc == 7)),
                                r=(wkey, (pfx + "xm", kc)), w=(("ps", b),))
                    ti = ntmp()
                    S.op("act", C("activation", out=tmp[:, ti, :NT], in_=ps[ba_][:, :NT], func=AF.Silu),
                         r=(("ps", ba_),), w=(("tmp", ti),))
                    S.op("dve", C("tensor_tensor", out=st.h[:, m, :], in0=tmp[:, ti, :NT], in1=ps[bb_][:, :NT], op=ALU.mult),
                         r=(("tmp", ti), ("ps", bb_)), w=((pfx + "h", m), pfx + "HR"))
            if cp == 3 and f == 0 and sts[0] is P and P.prefetch is not None:
                P.prefetch()
        S.op("act", C("activation", out=junk[:, 3:4], in_=onet[:, 0:1], func=AF.Sqrt), r=("onet",), w=("junkact",))
        if after_u is not None:
            after_u()
        for n in range(8):
            wap, wkey = wchunk(("D", f, n))
            wv = wap[:, 0:CH_D].rearrange("p (k j) -> p k j", k=NM)
            for st in sts:
                pfx, NT = st.pfx, st.NT
                b = nb()
                for kc in range(NM):
                    S.op("pe", C("matmul",
                        ps[b][:, :NT], lhsT=wv[:, kc, :], rhs=st.h[:, kc, :], start=(kc == 0), stop=(kc == NM - 1)),
                        r=(wkey, (pfx + "h", kc), pfx + "HR"), w=(("ps", b),))
                resid(st, n, b, l)
            if EARLY_T1 and n == 3 and len(sts) == 1:
                ln_T1_early(sts[0])

    def mixer(st):
        pfx, NT, GS = st.pfx, st.NT, st.GS
        HR = pfx + "HR"

        def in_mm(ci, mm, wv, wkey):
            b = nb()
            for kc in range(8):
                S.op("pe", C("matmul", ps[b][:, :NT], lhsT=wv[:, mm, kc, :], rhs=st.xm[:, kc, :], start=(kc == 0), stop=(kc == 7)),
                     r=(wkey, (pfx + "xm", kc)), w=(("ps", b),))
            if ci == 0:
                S.op("act", C("activation", out=st.xpre[:, mm, 3:3 + NT], in_=ps[b][:, :NT], func=AF.Identity),
                     r=(("ps", b),), w=((pfx + "xpre", mm), HR))
            elif ci == 1:
                S.op("act", C("activation", out=st.gg[:, mm, :], in_=ps[b][:, :NT], func=AF.Identity),
                     r=(("ps", b),), w=((pfx + "gg", mm), HR))
            else:
                S.op("dve", C("tensor_copy", out=st.us[:, mm, :], in_=ps[b][:, :NT]),
                     r=(("ps", b),), w=((pfx + "us", mm), HR))

        def gelu_pair(m0):
            S.op("act", C("activation", out=st.gg[:, m0:m0 + 2, :], in_=st.gg[:, m0:m0 + 2, :], func=AF.Gelu),
                 r=((pfx + "gg", m0), (pfx + "gg", m0 + 1)), w=((pfx + "gg", m0), (pfx + "gg", m0 + 1)))

        def chain_steps(m):
            q = m % 2
            cv = mt[:, q, 0 if m < 2 else 1, :NT]
            kcv = ("cvA" if m < 2 else "cvB", q)
            ra, ii, yy, hl = (mt[:, q, i, :NT] for i in (2, 3, 4, 5))
            xb = xcb[:, q, :NT]
            xk = (pfx + "xpre", m)
            hk = (pfx + "hist", m)
            st_ = {}
            steps = []
            steps.append(lambda: S.op("dve", C("tensor_copy", out=st.xpre[:, m, 0:3], in_=st.hist[:, m, :]), r=(hk, xk), w=(xk,)))
            steps.append(lambda: S.op("dve", C("tensor_scalar", out=cv, in0=st.xpre[:, m, 0:NT], scalar1=lrup[:, m, 0:1],
                                               scalar2=lrup[:, m, 4:5], op0=ALU.mult, op1=ALU.add),
                                      r=(xk, "lrup"), w=(kcv,)))
            for k in range(1, 4):
                steps.append(lambda k=k: S.op("dve", C("scalar_tensor_tensor", out=cv, in0=st.xpre[:, m, k:k + NT], scalar=lrup[:, m, k:k + 1],
                                                       in1=cv, op0=ALU.mult, op1=ALU.add),
                                              r=(xk, "lrup", kcv), w=(kcv,)))
            steps.append(lambda: S.op("dve", C("tensor_copy", out=st.hist[:, m, :], in_=st.xpre[:, m, st.NTr:st.NTr + 3]), r=(xk,), w=(hk,)))
            steps.append(lambda: S.op("act", C("activation", out=xb, in_=cv, func=AF.Identity), r=(kcv,), w=(("xcb", q),)))

            def gates():
                st_["bR"], st_["bI"] = nb(), nb()
                held_banks.update((st_["bR"], st_["bI"]))
                S.op("pe", C("matmul", ps[st_["bR"]][:, :NT], lhsT=wabd[:, m, :], rhs=xb, start=True, stop=True),
                     r=(("xcb", q), "wabd"), w=(("ps", st_["bR"]),))
                S.op("pe", C("matmul", ps[st_["bI"]][:, :NT], lhsT=wxbd[:, m, :], rhs=xb, start=True, stop=True),
                     r=(("xcb", q), "wxbd"), w=(("ps", st_["bI"]),))
            steps.append(gates)
            steps.append(lambda: S.op("act", C("activation", out=ra, in_=ps[st_["bR"]][:, :NT], func=AF.Sigmoid, bias=lrup[:, m, 5:6], scale=1.0),
                                      r=(("ps", st_["bR"]), "lrup"), w=(("ra", q),)))
            steps.append(lambda: S.op("act", C("activation", out=ii, in_=ps[st_["bI"]][:, :NT], func=AF.Sigmoid, bias=lrup[:, m, 6:7], scale=1.0),
                                      r=(("ps", st_["bI"]), "lrup"), w=(("ii", q),)))
            steps.append(lambda: held_banks.difference_update((st_["bR"], st_["bI"])))
            steps.append(lambda: S.op("act", C("activation", out=ra, in_=ra, func=AF.Exp, scale=csp[:, m:m + 1]),
                                      r=(("ra", q), "csp"), w=(("ra", q),)))
            steps.append(lambda: S.op("act", C("activation", out=yy, in_=ra, func=AF.Square), r=(("ra", q),), w=(("yy", q),)))
            steps.append(lambda: S.op("pool", C("tensor_tensor", out=ii, in0=ii, in1=cv, op=ALU.mult), r=(("ii", q), kcv), w=(("ii", q),)))
            steps.append(lambda: S.op("act", C("activation", out=yy, in_=yy, func=AF.Relu, bias=onet, scale=-1.0),
                                      r=(("yy", q), "onet"), w=(("yy", q),)))
            steps.append(lambda: S.op("act", C("activation", out=yy, in_=yy, func=AF.Sqrt, bias=tinyt, scale=1.0),
                                      r=(("yy", q), "tiny"), w=(("yy", q),)))
            steps.append(lambda: S.op("dve", C("tensor_tensor", out=ii, in0=ii, in1=yy, op=ALU.mult), r=(("ii", q), ("yy", q)), w=(("ii", q),)))
            steps.append(lambda: S.op("dve", C("tensor_tensor_scan", out=hl, data0=ra, data1=ii, initial=st.hst[:, m:m + 1],
                                               op0=ALU.mult, op1=ALU.add),
                                      r=(("ra", q), ("ii", q), (pfx + "hst", m)), w=(("hl", q),)))
            steps.append(lambda: S.op("dve", C("tensor_copy", out=st.hst[:, m:m + 1], in_=hl[:, st.NTr - 1:st.NTr]), r=(("hl", q),), w=((pfx + "hst", m),)))
            steps.append(lambda: S.op("pool", C("tensor_tensor", out=st.ymix[:, m, :], in0=hl, in1=st.gg[:, m, :], op=ALU.mult),
                                      r=(("hl", q), (pfx + "gg", m), HR), w=((pfx + "ymix", m),)))
            return steps

        chain_cache = {}

        def zip_chains(m0, m1, lo=0, hi=None):
            for m in (m0, m1):
                if m not in chain_cache:
                    chain_cache[m] = chain_steps(m)
            s0, s1 = chain_cache[m0], chain_cache[m1]
            for i in range(lo, hi if hi is not None else len(s0)):
                s0[i]()
                s1[i]()

        def vbranch(g, wv, wkey):
            gsl = slice(g * GS, (g + 1) * GS)
            b = nb()
            hb = stat_ctr[0] % 2
            stat_ctr[0] += 1
            for kc in range(8):
                S.op("pe", C("matmul", ps[b][:GS, :], lhsT=st.xm[:, kc, gsl], rhs=wv[:, kc, :], start=(kc == 0), stop=(kc == 7)),
                     r=(wkey, (pfx + "xm", kc)), w=(("ps", b),))
            S.op("dve", C("bn_stats", st6[:GS, hb, 0, :], ps[b][:GS, :]), r=(("ps", b),), w=(("st6", hb),))
            S.op("dve", C("bn_aggr", mv[:GS, hb, :], st6[:GS, hb, 0, :]), r=(("st6", hb),), w=(("mv", hb),))
            S.op("act", C("activation", out=sdv[:GS, hb, :], in_=mv[:GS, hb, 1:2], func=AF.Sqrt, bias=epst[:GS, :], scale=1.0),
                 r=(("mv", hb), "eps"), w=(("sdv", hb),))
            S.op("dve", C("reciprocal", rsv[:GS, hb, :], sdv[:GS, hb, :]), r=(("sdv", hb),), w=(("rsv", hb),))
            S.op("dve", C("scalar_tensor_tensor", out=nmr[:GS, hb, :], in0=mv[:GS, hb, 0:1], scalar=-1.0, in1=rsv[:GS, hb, :],
                          op0=ALU.mult, op1=ALU.mult),
                 r=(("mv", hb), ("rsv", hb)), w=(("nmr", hb),))
            ti = ntmp()
            S.op("act", C("activation", out=tmp[:GS, ti, :], in_=ps[b][:GS, :], func=AF.Identity,
                          bias=nmr[:GS, hb, :], scale=rsv[:GS, hb, :]),
                 r=(("ps", b), ("rsv", hb), ("nmr", hb)), w=(("tmp", ti),))
            S.op("dve", C("tensor_tensor", out=tmp[:GS, ti, :], in0=tmp[:GS, ti, :], in1=sgG[:GS, :], op=ALU.mult),
                 r=(("tmp", ti), "sgG"), w=(("tmp", ti),))
            if st is Sm:
                S.op("pool", C("tensor_tensor", out=tmp[:GS, ti, :], in0=tmp[:GS, ti, :], in1=sgB[:GS, :], op=ALU.add),
                     r=(("tmp", ti), "sgB"), w=(("tmp", ti),))
                S.op("pool", C("dma_start", out=vs_d, in_=tmp[:DEC, ti, :]), r=(("tmp", ti),), dma="ovs")
                S.op("dve", C("tensor_copy", out=st.vnb[:GS, g, :], in_=tmp[:GS, ti, :]),
                     r=(("tmp", ti),), w=((pfx + "vnb", g),))
            else:
                S.op("dve", C("tensor_tensor", out=st.vnb[:GS, g, :], in0=tmp[:GS, ti, :], in1=sgB[:GS, :], op=ALU.add),
                     r=(("tmp", ti), "sgB"), w=((pfx + "vnb", g),))

        wap, wk0 = wchunk(("I", 0))
        wv0 = wap.rearrange("p (mm k j) -> p mm k j", mm=4, k=8)
        for mm in range(4):
            in_mm(0, mm, wv0, wk0)
        zip_chains(0, 1, 0, 6)
        zip_chains(2, 3, 0, 6)
        wap, wk1 = wchunk(("I", 1))
        wv1 = wap.rearrange("p (mm k j) -> p mm k j", mm=4, k=8)
        for mm in range(4):
            in_mm(1, mm, wv1, wk1)
        zip_chains(0, 1, 6, 12)
        wap, wk2 = wchunk(("I", 2))
        wv2 = wap.rearrange("p (mm k j) -> p mm k j", mm=4, k=8)
        for mm in range(2):
            in_mm(2, mm, wv2, wk2)
        zip_chains(2, 3, 6, 8)
        zip_chains(0, 1, 12, 16)
        S.op("act", C("activation", out=st.gg[:, 0:4, :], in_=st.gg[:, 0:4, :], func=AF.Gelu),
             r=tuple((pfx + "gg", m) for m in range(4)), w=tuple((pfx + "gg", m) for m in range(4)))
        zip_chains(0, 1, 16)
        for mm in range(2, 4):
            in_mm(2, mm, wv2, wk2)
        zip_chains(2, 3, 8, 12)
        wap, wkv = wchunk(("V",))
        wvv = wap.rearrange("p (k n) -> p k n", k=8)
        for g in range(st.NG):
            vbranch(g, wvv, wkv)
            if g == 0:
                zip_chains(2, 3, 12)
        for grp in range(4):
            b = nb()
            for g in range(st.NG):
                S.op("pe", C("matmul", ps[b][:, g * GS:(g + 1) * GS], lhsT=st.vnb[:GS, g, grp * 128:(grp + 1) * 128],
                             rhs=wsT[:GS, grp, :GS], start=True, stop=True),
                     r=((pfx + "vnb", g), "wsT"), w=(("ps", b),))
            ti = ntmp()
            for g in range(st.NG):
                S.op("dve", C("tensor_tensor", out=tmp[:, ti, g * GS:(g + 1) * GS], in0=ps[b][:, g * GS:(g + 1) * GS],
                              in1=sgb[:, grp, :GS], op=ALU.add),
                     r=(("ps", b), "sgbraw"), w=(("tmp", ti),))
            S.op("pool", C("tensor_tensor", out=st.ymix[:, 4 + grp, :], in0=tmp[:, ti, :NT], in1=st.us[:, grp, :], op=ALU.mult),
                 r=(("tmp", ti), (pfx + "us", grp), HR), w=((pfx + "ymix", 4 + grp),))
        for ci in range(2):
            wap, wkey = wchunk(("O", ci))
            wv = wap.rearrange("p (nn k j) -> p nn k j", nn=4, k=8)
            for nn in range(4):
                n = 4 * ci + nn
                b = nb()
                for kc in range(8):
                    S.op("pe", C("matmul", ps[b][:, :NT], lhsT=wv[:, nn, kc, :], rhs=st.ymix[:, kc, :], start=(kc == 0), stop=(kc == 7)),
                         r=(wkey, (pfx + "ymix", kc)), w=(("ps", b),))
                resid(st, n, b, 1)
            if EARLY_T1 and ci == 0:
                ln_T1_early(st)

    out_sems = set()
    def lnin_A(st, g, t0):
        halves, src, skey = ln_xin(st, g, t0)
        ln_norm(st, g, halves, src, skey)
        return (g, src, skey)

    def body(st, p, after_u2=None):
        ffn([st], 0, 0)
        S.mark("ffn1")
        ln_mid(st, 1)
        S.mark("ln1")
        mixer(st)
        S.mark("mixer")
        ln_mid(st, 2)
        S.mark("ln2")
        ffn([st], 1, 2, after_u=after_u2)
        S.mark("ffn2")

    def final_copy(st, g, halves):
        GS = st.GS
        slot = ntmr()
        dst, dkey = tmr[:, slot, :], ("tmr", slot)
        for half, (hap, hkey) in enumerate(halves):
            S.op("dve", C("tensor_copy", out=dst[:GS, half * 512:(half + 1) * 512], in_=hap), r=(hkey,), w=((dkey, half), dkey))
        release_banks(halves)
        return slot, dst, dkey

    def final_rest(st, g, fc, t0):
        slot, dst, dkey = fc
        GS = st.GS
        halves = [(dst[:GS, 0:512], (dkey, 0)), (dst[:GS, 512:1024], (dkey, 1))]
        ln_norm(st, g, halves, dst, dkey)
        ln_out(st, g, dst, dkey, slot, t0)

    def load_sample_states():
        S.op("sp", C("dma_start", out=Sm.hst, in_=slru_d.rearrange("(m p) -> p m", p=128),
                                         allow_slow_non_contiguous=True),
             w=tuple(("Shst", m) for m in range(4)), dma="cst4")
        for m in range(4):
            S.op("sp", C("dma_start", out=Sm.hist[:, m, :], in_=sconv_d[:, m * 128:(m + 1) * 128].rearrange("k p -> p k"),
                         allow_slow_non_contiguous=True),
                 w=(("Shist", m),), dma="cst5")
        S.fence([("Shist", m) for m in range(4)], "cst5")

    P.prefetch = None
    do_s = not os.environ.get("KNOS")
    do_p = not os.environ.get("KONLYS")
    if do_p:
        inA = {g: lnin_A(P, g, 0) for g in range(3)}
        for g in range(4):
            ln_T2(P, *inA[g], 0)
            if g == 0:
                load_xg(P, 0, 3, xp_d)
                inA[3] = lnin_A(P, 3, 0)
        S.mark("ln_in")
        for p in range(NPASS):
            t0 = p * NTP
            nxt = p + 1 < NPASS
            nstate = {}
            if nxt:
                def pf(p=p):
                    for g in range(3):
                        load_xg(P, (p + 1) * NTP, g, xp_d)
                P.prefetch = pf

                def au(p=p, nstate=nstate):
                    for g in range(3):
                        nstate[g] = lnin_A(P, g, (p + 1) * NTP)
            elif do_s:
                def pf_s():
                    load_xg(Sm, 0, 0, xs_d)
                    load_sample_states()
                P.prefetch = pf_s

                def au(nstate=nstate):
                    nstate["s"] = lnin_A(Sm, 0, 0)
            else:
                P.prefetch = None
                au = None
            body(P, p, after_u2=au)
            fcs = {}
            for g in range(4):
                halves = ln_T1(P, g)
                fcs[g] = final_copy(P, g, halves)
                if nxt:
                    ln_T2(P, *nstate[g], 0)
                    if g == 0:
                        load_xg(P, t0 + NTP, 3, xp_d)
                    if g == 1:
                        nstate[3] = lnin_A(P, 3, t0 + NTP)
            if not nxt and do_s:
                ln_T2(Sm, *nstate["s"], 0)
            for g in range(4):
                final_rest(P, g, fcs[g], t0)
    elif do_s:
        load_xg(Sm, 0, 0, xs_d)
        load_sample_states()
        ia = lnin_A(Sm, 0, 0)
        ln_T2(Sm, *ia, 0)
    if do_s:
        P.prefetch = None
        body(Sm, 0)
        halves = ln_T1(Sm, 0)
        final_rest(Sm, 0, final_copy(Sm, 0, halves), 0)
    if os.environ.get("KDBG") == "xa":
        S.cut = False
        S.op("sp", C("dma_start", out=yp_d.rearrange("(a b) d -> a (b d)", a=128), in_=P.xa.rearrange("p c t -> p (c t)")),
             r=tuple(("Pxa", c, g) for c in range(8) for g in range(4)), dma="dbg")
        S.cut = True
    for st, cd, ld, nm in ((P, convp_d, lrup_o, "p"), (Sm, convs_d, lrus_o, "s")):
        NT = st.NT
        for m in range(4):
            S.op("sp", C("dma_start", out=cd[:, m * 128:(m + 1) * 128].rearrange("k p -> p k"), in_=st.hist[:, m, :],
                         allow_slow_non_contiguous=True),
                 r=((st.pfx + "hist", m),), dma="oc" + nm)
        S.op("sp", C("dma_start", out=ld.rearrange("(m p) -> p m", p=128), in_=st.hst, allow_slow_non_contiguous=True),
             r=tuple((st.pfx + "hst", m) for m in range(4)), dma="ol" + nm)

    S.trailers = {}
    for en in os.environ.get("KTRAIL", "").split(","):
        if en == "act":
            S.trailers["act"] = lambda e: e.activation(out=junk[:, 0:1], in_=epst[:, 0:1], func=AF.Identity)
        if en == "dve":
            S.trailers["dve"] = lambda e: e.tensor_copy(out=junk[:, 1:2], in_=epst[:, 0:1])
        if en == "pool":
            S.trailers["pool"] = lambda e: e.tensor_copy(out=junk[:, 2:3], in_=epst[:, 0:1])
    S.number()
    sem_names = ["E_pe", "E_act", "E_dve", "E_pool", "E_sp"] + sorted(S.dcnt.keys())
    sems = {n: es.enter_context(nc.semaphore(n.replace("_", ""))) for n in sem_names}
    outs_act = [f"tmr{i}" for i in range(4)] + ["ovs"]
    outs_sp = ["ocp", "olp", "ocs", "ols", "dbg"]
    with nc.Block() as block:
        @block.tensor
        def _(e):
            S.emit("pe", e, sems)

        @block.scalar
        def _(e):
            S.emit("act", e, sems)

        @block.vector
        def _(e):
            S.emit("dve", e, sems)

        @block.gpsimd
        def _(e):
            S.emit("pool", e, sems, final_waits=outs_act)

        @block.sync
        def _(e):
            S.emit("sp", e, sems, final_waits=outs_sp)
    es.close()
    return nc


def _layout_weights(inp):
    parts = {}
    for f, (n1, n3, n2) in enumerate((("ffn1_w1", "ffn1_w3", "ffn1_w2"), ("ffn2_w1", "ffn2_w3", "ffn2_w2"))):
        w1 = np.asarray(inp[n1][0]).reshape(8, 128, NM, 128)
        w3 = np.asarray(inp[n3][0]).reshape(8, 128, NM, 128)
        u = np.stack([w1, w3], axis=0)
        u = u.transpose(2, 3, 0, 1, 4)
        u = u.reshape(128, 11, 2, 2, 8, 128)
        for cp in range(11):
            parts[("U", f, cp)] = u[:, cp].reshape(128, -1)
        w2 = np.asarray(inp[n2][0]).reshape(NM, 128, 8, 128)
        w2 = w2.transpose(1, 2, 0, 3)
        for n in range(8):
            parts[("D", f, n)] = w2[:, n].reshape(128, -1)
    wi = np.asarray(inp["w_in"][0])
    wm = wi[:, :1536].reshape(8, 128, 12, 128).transpose(1, 2, 0, 3)
    for ci in range(3):
        parts[("I", ci)] = wm[:, 4 * ci:4 * ci + 4].reshape(128, -1)
    parts[("V",)] = wi[:, 1536:].reshape(8, 128, 512).transpose(1, 0, 2).reshape(128, -1)
    wo = np.asarray(inp["w_out"][0]).reshape(8, 128, 8, 128).transpose(1, 2, 0, 3)
    for ci in range(2):
        parts[("O", ci)] = wo[:, 4 * ci:4 * ci + 4].reshape(128, -1)
    return np.ascontiguousarray(np.concatenate([parts[k] for k in CH_ORDER], axis=1), dtype=np.float32)


def _fm(v, nchunk):
    return np.asarray(v).reshape(nchunk, 128).T


def make_in_maps(inp, T):
    wall = _layout_weights(inp)
    lnp = np.concatenate([_fm(inp["ln_in_g"], 8), _fm(inp["ln_in_b"], 8),
                          _fm(inp["ln_g"][0, 0], 8), _fm(inp["ln_b"][0, 0], 8),
                          _fm(inp["ln_g"][0, 1], 8), _fm(inp["ln_b"][0, 1], 8),
                          _fm(inp["ln_g"][0, 2], 8), _fm(inp["ln_b"][0, 2], 8)], axis=1)
    cw = np.asarray(inp["conv_w"][0])
    q = [_fm(cw[k], 4) for k in range(4)] + [_fm(inp["conv_b"][0], 4), _fm(inp["lru_ba"][0], 4),
                                            _fm(inp["lru_bx"][0], 4), _fm(inp["lru_lambda"][0], 4)]
    lrup = np.stack(q, axis=2).reshape(128, 32)

    def gl(w):
        w = np.asarray(w[0]).reshape(4, 2, 64, 64)
        return w.transpose(1, 2, 0, 3).reshape(128, 256)
    sgw = np.asarray(inp["sg_w"][0]).transpose(2, 0, 1).reshape(128, 512)
    shared = {
        "wada": np.ascontiguousarray(inp["w_ada"][0], dtype=np.float32),
        "bada": np.ascontiguousarray(_fm(inp["b_ada"][0], 72), dtype=np.float32),
        "lnp": np.ascontiguousarray(lnp, dtype=np.float32),
        "lrup": np.ascontiguousarray(lrup, dtype=np.float32),
        "wag": np.ascontiguousarray(gl(inp["lru_wa"]), dtype=np.float32),
        "wxg": np.ascontiguousarray(gl(inp["lru_wx"]), dtype=np.float32),
        "sgw": np.ascontiguousarray(sgw, dtype=np.float32),
        "sgb": np.ascontiguousarray(np.asarray(inp["sg_b"][0]).reshape(1, 512), dtype=np.float32),
        "sgln": np.ascontiguousarray(np.stack([inp["sg_ln_g"][0], inp["sg_ln_b"][0]]), dtype=np.float32),
        "ln3": np.ascontiguousarray(np.stack([inp["ln_g"][0, 2], inp["ln_b"][0, 2]]), dtype=np.float32),
        "wall": wall,
    }
    maps = []
    for b in range(8):
        ccb = np.stack([_fm(inp["c_prompt"][b], 8), _fm(inp["c_sample"][b], 8)], axis=2).reshape(128, 16)
        m = dict(shared)
        m["xp"] = np.ascontiguousarray(inp["x_prompt"][b, :T], dtype=np.float32)
        m["xs"] = np.ascontiguousarray(inp["x_sample"][b], dtype=np.float32)
        m["cc"] = np.ascontiguousarray(ccb, dtype=np.float32)
        m["sconv"] = np.ascontiguousarray(inp["state_conv"][0, b], dtype=np.float32)
        m["slru"] = np.ascontiguousarray(inp["state_lru"][0, b], dtype=np.float32)
        maps.append(m)
    return maps


def run(inp, T=SEQ, cores=8, trace=False):
    nc = build_nc(T)
    maps = make_in_maps(inp, T)[:cores]
    res = run_bass_kernel_spmd(nc, maps, core_ids=list(range(cores)), trace=trace)
    r = res.results
    yp = np.stack([r[b]["yp"] for b in range(cores)])
    ys = np.stack([r[b]["ys"] for b in range(cores)])
    convp = np.stack([r[b]["convp"] for b in range(cores)])[None]
    lrup = np.stack([r[b]["lrupo"] for b in range(cores)])[None]
    convs = np.stack([r[b]["convs"] for b in range(cores)])[None]
    lrus = np.stack([r[b]["lruso"] for b in range(cores)])[None]
    vs = np.stack([r[b]["vs"] for b in range(cores)])[None]
    outs = tuple(np.ascontiguousarray(a, dtype=np.float32) for a in (yp, ys, convp, lrup, convs, lrus, vs))
    return outs, res


def kernel(**inputs):
    inp = {k: np.asarray(v) for k, v in inputs.items()}
    outs, _ = run(inp, SEQ, 8)
    return outs
```
